# Optimizing a Trainium2 kernel written in Bass

```python
import jax, jax.numpy as jnp
from jax import lax
import numpy as np

D_MODEL = 2048
BATCH = 8
SEQ = 4096
DEPTH = 2

CHUNK = 64
GMLP_BLOCK = 128
MIX_WIDTH = D_MODEL
A_WIDTH = MIX_WIDTH // 2
B_WIDTH = MIX_WIDTH - A_WIDTH
A_HEADS = 8
A_HEAD_DIM = A_WIDTH // A_HEADS
B_GROUPS = 8
CONV_WIDTH = 3
IN_COLS = 2 * A_WIDTH + 3 * B_WIDTH
D_FF = -(-8 * D_MODEL // (3 * 256)) * 256
RMS_EPS = 1e-6
LN_EPS = 1e-5

kernel_name = "hybrid_gmlp_shortconv_swiglu_trunk"


def rmsnorm(x, g):
    xf = x.astype(jnp.float32)
    y = xf * lax.rsqrt(jnp.mean(xf * xf, axis=-1, keepdims=True) + RMS_EPS)
    return (y * g.astype(jnp.float32)).astype(x.dtype)


def layernorm(x, g, b):
    xf = x.astype(jnp.float32)
    mu = jnp.mean(xf, axis=-1, keepdims=True)
    xc = xf - mu
    y = xc * lax.rsqrt(jnp.mean(xc * xc, axis=-1, keepdims=True) + LN_EPS)
    return (y * g.astype(jnp.float32) + b.astype(jnp.float32)).astype(x.dtype)


def chunk_causal_block_mask():
    pos = jnp.arange(GMLP_BLOCK)
    return (pos[None, :] // CHUNK) <= (pos[:, None] // CHUNK)


def spatial_gating(u, v, w_s, b_s, ln_g, ln_b):
    bsz, seq, _ = v.shape
    v = layernorm(v, ln_g, ln_b)
    v = v.reshape(bsz, seq // GMLP_BLOCK, GMLP_BLOCK, A_HEADS, A_HEAD_DIM)
    w = jnp.where(chunk_causal_block_mask()[None], w_s, jnp.zeros((), w_s.dtype))
    mixed = jnp.einsum('hij,bcjhd->bcihd', w, v) + b_s.T[None, None, :, :, None]
    return u * mixed.reshape(bsz, seq, A_WIDTH)


def causal_depthwise_conv(h, w):
    c = h.shape[-1]
    return lax.conv_general_dilated(
        h, w[:, None, :].astype(h.dtype), window_strides=(1,),
        padding=[(CONV_WIDTH - 1, 0)],
        dimension_numbers=('NWC', 'WIO', 'NWC'),
        feature_group_count=c)


def hybrid_layer(x, norm1_g, w_in, ln_g, ln_b, w_s, b_s, conv_w, group_norm_g,
                 w_out, norm2_g, w_gate, w_up, w_down):
    h = rmsnorm(x, norm1_g)
    z = jnp.einsum('bsd,dn->bsn', h, w_in)
    z_a = jax.nn.gelu(z[..., :2 * A_WIDTH])
    u, v = z_a[..., :A_WIDTH], z_a[..., A_WIDTH:]
    off = 2 * A_WIDTH
    gate_b = z[..., off:off + B_WIDTH]
    gate_c = z[..., off + B_WIDTH:off + 2 * B_WIDTH]
    h_b = z[..., off + 2 * B_WIDTH:]
    y_a = spatial_gating(u, v, w_s, b_s, ln_g, ln_b)
    y_b = gate_b * causal_depthwise_conv(gate_c * h_b, conv_w)
    y = jnp.concatenate([rmsnorm(y_a, group_norm_g[:A_WIDTH]),
                         rmsnorm(y_b, group_norm_g[A_WIDTH:])], axis=-1)
    x = x + jnp.einsum('bsm,md->bsd', y, w_out)
    h2 = rmsnorm(x, norm2_g)
    act = jax.nn.silu(jnp.einsum('bsd,df->bsf', h2, w_gate)) * jnp.einsum('bsd,df->bsf', h2, w_up)
    return x + jnp.einsum('bsf,fd->bsd', act, w_down)


def setup_inputs(seed: int = 0) -> dict:
    key = jax.random.key(seed)
    ks = jax.random.split(key, 16)
    f32 = jnp.float32
    nrm = lambda k, shape, scale: jax.random.normal(k, shape, f32) * scale
    return {
        "x": jax.random.normal(ks[0], (BATCH, SEQ, D_MODEL), f32),
        "norm1_g": 1.0 + nrm(ks[1], (DEPTH, D_MODEL), 0.02),
        "w_in": nrm(ks[2], (DEPTH, D_MODEL, IN_COLS), D_MODEL ** -0.5),
        "gmlp_ln_g": 1.0 + nrm(ks[3], (DEPTH, A_WIDTH), 0.02),
        "gmlp_ln_b": nrm(ks[4], (DEPTH, A_WIDTH), 0.02),
        "w_spatial": nrm(ks[5], (DEPTH, A_HEADS, GMLP_BLOCK, GMLP_BLOCK), GMLP_BLOCK ** -0.5),
        "b_spatial": 1.0 + nrm(ks[6], (DEPTH, A_HEADS, GMLP_BLOCK), 0.1),
        "conv_w": nrm(ks[7], (DEPTH, CONV_WIDTH, B_WIDTH), CONV_WIDTH ** -0.5),
        "group_norm_g": 1.0 + nrm(ks[8], (DEPTH, MIX_WIDTH), 0.02),
        "w_out": nrm(ks[9], (DEPTH, MIX_WIDTH, D_MODEL), MIX_WIDTH ** -0.5),
        "norm2_g": 1.0 + nrm(ks[10], (DEPTH, D_MODEL), 0.02),
        "w_gate": nrm(ks[11], (DEPTH, D_MODEL, D_FF), D_MODEL ** -0.5),
        "w_up": nrm(ks[12], (DEPTH, D_MODEL, D_FF), D_MODEL ** -0.5),
        "w_down": nrm(ks[13], (DEPTH, D_FF, D_MODEL), D_FF ** -0.5),
        "final_norm_g": 1.0 + nrm(ks[14], (D_MODEL,), 0.02),
    }


def reference(x, norm1_g, w_in, gmlp_ln_g, gmlp_ln_b, w_spatial, b_spatial, conv_w,
              group_norm_g, w_out, norm2_g, w_gate, w_up, w_down, final_norm_g):
    for layer in range(DEPTH):
        x = hybrid_layer(x, norm1_g[layer], w_in[layer], gmlp_ln_g[layer], gmlp_ln_b[layer],
                         w_spatial[layer], b_spatial[layer], conv_w[layer], group_norm_g[layer],
                         w_out[layer], norm2_g[layer], w_gate[layer], w_up[layer], w_down[layer])
    return rmsnorm(x, final_norm_g)
```

```python
import numpy as np
from contextlib import ExitStack

import concourse.bass as bass
import concourse.mybir as mybir
from concourse.bass_utils import run_bass_kernel_spmd

F32 = mybir.dt.float32
BF16 = mybir.dt.bfloat16
AF = mybir.ActivationFunctionType
ALU = mybir.AluOpType

D = 2048
NCH = 16
TT = 512
AW = 1024
NF = 44
DFF = 5632
L = 2
SEQ = 4096
RMS_EPS = 1e-6
LN_EPS = 1e-5
NS = 4
SLOT = 8192
NCAST = 8

ITEM_GROUPS = [("wv", 2, 8192), ("wu", 2, 8192), ("bch", 8, 6144), ("wo", 4, 8192),
               ("ff", 22, 8192), ("wd", 16, 5632)]
ITEMS = []
_off = 0
for _n, _c, _s in ITEM_GROUPS:
    for _i in range(_c):
        ITEMS.append((_n, _i, _off, _s))
        _off += _s
LAYER_ELEMS = _off
ITEM_IDX = {(n, i): k for k, (n, i, o, s) in enumerate(ITEMS)}
NITEMS = len(ITEMS)

C_N1G, C_LNG, C_LNB, C_CW, C_GNG, C_N2G = 0, 16, 24, 32, 56, 72
C_PER_L = 88
C_FNG = L * C_PER_L
NCOLS = C_FNG + 16
SM_BS = NCOLS
SM_WS = NCOLS + L * 8 * 128
SM_TOT = SM_WS + L * 8 * 128


class _Op:
    __slots__ = ("eng", "fn", "reads", "writes", "dma", "deps", "needs_inc", "semval")

    def __init__(self, eng, fn, reads, writes, dma):
        self.eng = eng
        self.fn = fn
        self.reads = tuple(reads)
        self.writes = tuple(writes)
        self.dma = dma
        self.deps = ()
        self.needs_inc = False
        self.semval = None


class Sched:
    def __init__(self):
        self.ops = []

    def op(self, eng, fn, reads=(), writes=(), dma=None):
        self.ops.append(_Op(eng, fn, reads, writes, dma))

    def analyze(self):
        last_w = {}
        readers = {}
        ops = self.ops
        for i, op in enumerate(ops):
            raw = set()
            other = set()
            for r in op.reads:
                w = last_w.get(r)
                if w is not None:
                    raw.add(w)
            for w_ in op.writes:
                w = last_w.get(w_)
                if w is not None:
                    other.add(w)
                rs = readers.get(w_)
                if rs:
                    other |= rs
            deps = set()
            for d in raw | other:
                if d == i:
                    continue
                dop = ops[d]
                if dop.eng == op.eng and dop.dma is None and op.dma is None:
                    if op.eng == "pe":
                        continue
                    if d not in raw:
                        continue
                deps.add(d)
            op.deps = tuple(sorted(deps))
            for r in op.reads:
                readers.setdefault(r, set()).add(i)
            for w_ in op.writes:
                last_w[w_] = i
                readers[w_] = set()
        for op in ops:
            for d in op.deps:
                ops[d].needs_inc = True
        cnt = {}
        for op in ops:
            if op.dma is not None:
                key = ("dma", op.dma)
                cnt[key] = cnt.get(key, 0) + 16
                op.semval = (key, cnt[key])
            elif op.needs_inc:
                key = ("eng", op.eng)
                cnt[key] = cnt.get(key, 0) + 1
                op.semval = (key, cnt[key])
        self.counts = cnt

    def emit_engine(self, eng_name, e, sems):
        waited = {}
        ops = self.ops
        for op in ops:
            if op.eng != eng_name:
                continue
            need = {}
            for d in op.deps:
                k, v = ops[d].semval
                if need.get(k, 0) < v:
                    need[k] = v
            for k, v in need.items():
                if waited.get(k, 0) >= v:
                    continue
                e.wait_ge(sems[k], v)
                waited[k] = v
            ins = op.fn(e)
            if op.semval is not None:
                ins.then_inc(sems[op.semval[0]], 16 if op.dma is not None else 1)


def build(n_tiles, n_layers=L, final_norm=True):
    nc = bass.Bass("TRN2", target_bir_lowering=False)
    xt = nc.dram_tensor("xt", [n_tiles, 128, NCH * TT], F32, kind="ExternalInput").ap()
    wsrc = [nc.dram_tensor("wsrc%d" % l, [128, LAYER_ELEMS], F32, kind="ExternalInput").ap() for l in range(L)]
    small = nc.dram_tensor("small", [128, SM_TOT], F32, kind="ExternalInput").ap()
    out = nc.dram_tensor("out", [n_tiles, 128, NCH * TT], F32, kind="ExternalOutput").ap()
    wbf = nc.dram_tensor("wbf", [128, L * LAYER_ELEMS], BF16, kind="Internal").ap()

    S = Sched()
    es = ExitStack()
    with es:
        def sb(name, shape, dt):
            return es.enter_context(nc.sbuf_tensor(name, shape, dt))

        x = sb("x", [128, NCH * TT], F32)
        h = sb("h", [128, NCH * TT], BF16)
        act = sb("act", [128, NF * TT], BF16)
        ring = [sb("ring%d" % i, [128, SLOT], BF16) for i in range(NS)]
        vtmp = [sb("vtmp%d" % i, [128, AW], F32) for i in range(2)]
        sq = [sb("sq%d" % i, [128, TT], BF16) for i in range(3)]
        rs1 = sb("rs1", [128, TT], F32)
        rsA = sb("rsA", [128, TT], F32)
        rsB = sb("rsB", [128, TT], F32)
        NTMP = 8
        tmp = [sb("tmp%d" % i, [128, TT + 2], F32) for i in range(NTMP)]
        halo = sb("halo", [128, L * 8 * 2], F32)
        cols = sb("cols", [128, NCOLS], F32)
        E = sb("E", [128, L * 8 * 128], F32)
        wsT = sb("wsT", [128, L * 8 * 128], BF16)
        ones = sb("ones", [128, 128], BF16)
        ones32 = sb("ones32", [128, 128], F32)
        st = [sb("st%d" % i, [128, 12], F32) for i in range(2)]
        mv = [sb("mv%d" % i, [128, 2], F32) for i in range(2)]
        sd = [sb("sd%d" % i, [128, 2], F32) for i in range(2)]
        ps = es.enter_context(nc.psum_tensor("ps", [128, 8 * TT], F32))

        sem_keys = [("eng", "pe"), ("eng", "act"), ("eng", "dve")]
        sem_keys += [("dma", ("slot", i)) for i in range(NS)]
        sem_keys += [("dma", ("cast", i)) for i in range(NCAST)]
        sem_keys += [("dma", "x"), ("dma", "out"), ("dma", "small0"), ("dma", "small1")]
        sems = {}
        for k in sem_keys:
            nm = "s_" + "_".join(str(t) for t in (k[1] if isinstance(k[1], tuple) else (k[1],)))
            sems[k] = es.enter_context(nc.semaphore(nm))
        block = es.enter_context(nc.Block())

        def xs(c):
            return x[:, c * TT:(c + 1) * TT]

        def hs(c):
            return h[:, c * TT:(c + 1) * TT]

        def acs(c):
            return act[:, c * TT:(c + 1) * TT]

        def bank(b):
            return ps[:, b * TT:(b + 1) * TT]

        def col(k):
            return cols[:, k:k + 1]

        bstate = {"next": 0, "held": set()}

        def getbank():
            while True:
                b = bstate["next"]
                bstate["next"] = (b + 1) % 8
                if b not in bstate["held"]:
                    return b

        rot = {}

        def nxt(name, n):
            v = rot.get(name, 0)
            rot[name] = (v + 1) % n
            return v

        ws = {"issued": 0, "total": n_tiles * n_layers * NITEMS, "released": set()}

        def item_of(q):
            r = q % (n_layers * NITEMS)
            return r // NITEMS, r % NITEMS

        def issue_load(q):
            l, it = item_of(q)
            n, i, off, size = ITEMS[it]
            s = q % NS
            o = l * LAYER_ELEMS + off
            S.op("sp", lambda e: e.dma_start(out=ring[s][:, 0:size], in_=wbf[:, o:o + size]),
                 reads=[("scr", l, it)], writes=[("slot", s)], dma=("slot", s))

        def pump():
            while ws["issued"] < ws["total"]:
                k = ws["issued"]
                if k >= NS and (k - NS) not in ws["released"]:
                    break
                issue_load(k)
                ws["issued"] += 1

        def acquire(q):
            pump()
            assert ws["issued"] > q, (q, ws["issued"])
            return q % NS

        def release(q):
            ws["released"].add(q)
            pump()

        def rms_begin():
            b = getbank()
            bstate["held"].add(b)
            return {"b": b, "n": 0}

        def rms_accum(stt, src_ap, src_key, total, defer=False):
            k = nxt("sq", 3)
            b = stt["b"]
            first = stt["n"] == 0
            last = stt["n"] == total - 1
            stt["n"] += 1
            S.op("act", lambda e: e.activation(out=sq[k][:], in_=src_ap, func=AF.Square),
                 reads=[src_key], writes=[("sq", k)])

            def pe_part():
                S.op("pe", lambda e: e.matmul(bank(b), lhsT=ones[:], rhs=sq[k][:], start=first, stop=last),
                     reads=[("sq", k), "ones"], writes=[("ps", b)])
            if defer:
                stt.setdefault("pending", []).append(pe_part)
            else:
                pe_part()

        def rms_flush(stt):
            for p in stt.get("pending", []):
                p()
            stt["pending"] = []

        def rms_finish(stt, dst, dkey, nfeat):
            rms_flush(stt)
            b = stt["b"]
            S.op("act", lambda e: e.activation(out=dst[:], in_=bank(b), func=AF.Sqrt,
                                               scale=1.0 / nfeat, bias=RMS_EPS),
                 reads=[("ps", b)], writes=[dkey])
            S.op("dve", lambda e: e.reciprocal(out=dst[:], in_=dst[:]), reads=[dkey], writes=[dkey])
            bstate["held"].discard(b)

        def norm_to_h(gcol0):
            stt = rms_begin()
            for c in range(NCH):
                rms_accum(stt, xs(c), ("x", c), NCH)
            rms_finish(stt, rs1, "rs1", D)
            for c in range(NCH):
                S.op("dve", lambda e, c=c: e.scalar_tensor_tensor(
                    out=hs(c), in0=xs(c), scalar=col(gcol0 + c), in1=rs1[:],
                    op0=ALU.mult, op1=ALU.mult),
                    reads=[("x", c), "rs1", "cols"], writes=[("h", c)])

        def mm_group(b, lhs_fn, rhs_fn, nk, reads, out_ap=None):
            o = bank(b) if out_ap is None else out_ap

            def fn(e):
                ins = None
                for kc in range(nk):
                    ins = e.matmul(o, lhsT=lhs_fn(kc), rhs=rhs_fn(kc), start=(kc == 0), stop=(kc == nk - 1))
                return ins
            S.op("pe", fn, reads=reads, writes=[("ps", b)])

        S.op("sp", lambda e: e.dma_start(out=cols[:], in_=small[:, 0:NCOLS]), writes=["cols"], dma="small0")
        S.op("sp", lambda e: e.dma_start(out=x[:, 0:SM_TOT - NCOLS], in_=small[:, NCOLS:SM_TOT]),
             writes=[("x", c) for c in range(8)], dma="small1")
        S.op("dve", lambda e: e.memset(ones[:], 1.0), writes=["ones"])
        S.op("dve", lambda e: e.memset(ones32[:], 1.0), writes=["ones32"])
        S.op("dve", lambda e: e.memset(halo[:], 0.0), writes=[("halo", 2 * i) for i in range(L * 8)])
        WS0 = L * 8 * 128
        S.op("dve", lambda e: e.memset(
            x[64:128, WS0:2 * WS0].rearrange("p (a b) -> p a b", b=128)[:, :, 0:64], 0.0),
            reads=[("x", c) for c in range(4, 8)], writes=[("x", c) for c in range(4, 8)])
        for qd in range(4):
            b = getbank()
            S.op("pe", lambda e, b=b, qd=qd: e.matmul(bank(b), lhsT=ones32[:], rhs=x[:, WS0 + qd * 512:WS0 + (qd + 1) * 512],
                                                    start=True, stop=True),
                 reads=["ones32", ("x", 4 + qd)], writes=[("ps", b)])
            for hh in range(4):
                lh = qd * 4 + hh
                l_, hd_ = lh // 8, lh % 8
                S.op("dve", lambda e, b=b, hh=hh, lh=lh, l_=l_, hd_=hd_: e.scalar_tensor_tensor(
                    out=E[:, lh * 128:(lh + 1) * 128], in0=bank(b)[:, hh * 128:(hh + 1) * 128],
                    scalar=col(l_ * C_PER_L + C_LNB + hd_), in1=x[:, lh * 128:(lh + 1) * 128],
                    op0=ALU.mult, op1=ALU.add),
                    reads=[("ps", b), "cols", ("x", lh // 4)], writes=["E"])
        S.op("dve", lambda e: e.tensor_copy(out=wsT[:], in_=x[:, WS0:2 * WS0]),
             reads=[("x", c) for c in range(4, 8)], writes=["wsT"])

        ci = 0
        for l in range(n_layers):
            for it, (n, i, off, size) in enumerate(ITEMS):
                o = l * LAYER_ELEMS + off
                r = ci % NCAST
                S.op("pool", lambda e, o=o, size=size, l=l, off=off: e.dma_start(out=wbf[:, o:o + size], in_=wsrc[l][:, off:off + size]),
                     writes=[("scr", l, it), ("castsem", r)], dma=("cast", r))
                ci += 1

        q = 0
        for ti in range(n_tiles):
            S.op("sp", lambda e, ti=ti: e.dma_start(out=x[:], in_=xt[ti]),
                 writes=[("x", c) for c in range(NCH)], dma="x")
            for l in range(n_layers):
                cb = l * C_PER_L
                norm_to_h(cb + C_N1G)

                qv0, qv1 = q, q + 1
                sv = [acquire(qv0), acquire(qv1)]
                q += 2
                for tb in range(4):
                    vb = tb % 2
                    for hf in range(2):
                        b = getbank()
                        s = sv[hf]
                        mm_group(b, lambda kc, tb=tb: h[:, kc * TT + tb * 128: kc * TT + (tb + 1) * 128],
                                 lambda kc, s=s: ring[s][:, kc * 512:(kc + 1) * 512], NCH,
                                 reads=[("h", c) for c in range(NCH)] + [("slot", s)])
                        S.op("act", lambda e, b=b, vb=vb, hf=hf: e.activation(
                            out=vtmp[vb][:, hf * 512:(hf + 1) * 512], in_=bank(b), func=AF.Gelu_apprx_tanh),
                            reads=[("ps", b)], writes=[("vtmp", vb, hf)])
                        S.op("dve", lambda e, vb=vb, hf=hf: e.bn_stats(
                            out=st[vb][:, hf * 6:(hf + 1) * 6], in_=vtmp[vb][:, hf * 512:(hf + 1) * 512]),
                            reads=[("vtmp", vb, hf)], writes=[("st", vb, hf)])
                    S.op("dve", lambda e, vb=vb: e.bn_aggr(out=mv[vb][:], in_=st[vb][:]),
                         reads=[("st", vb, 0), ("st", vb, 1)], writes=[("mv", vb)])
                    S.op("dve", lambda e, vb=vb: e.tensor_scalar(
                        out=sd[vb][:, 0:1], in0=mv[vb][:, 1:2], scalar1=LN_EPS, scalar2=None, op0=ALU.add),
                        reads=[("mv", vb)], writes=[("sd0", vb)])
                    S.op("act", lambda e, vb=vb: e.activation(out=sd[vb][:, 0:1], in_=sd[vb][:, 0:1], func=AF.Sqrt),
                         reads=[("sd0", vb)], writes=[("sd0", vb)])
                    S.op("dve", lambda e, vb=vb: e.reciprocal(out=sd[vb][:, 0:1], in_=sd[vb][:, 0:1]),
                         reads=[("sd0", vb)], writes=[("sd0", vb)])
                    S.op("dve", lambda e, vb=vb: e.tensor_scalar(
                        out=sd[vb][:, 1:2], in0=mv[vb][:, 0:1], scalar1=sd[vb][:, 0:1], scalar2=-1.0,
                        op0=ALU.mult, op1=ALU.mult),
                        reads=[("mv", vb), ("sd0", vb)], writes=[("sd1", vb)])
                    S.op("act", lambda e, vb=vb, tb=tb: e.activation(
                        out=act[:, (24 + 2 * tb) * TT:(26 + 2 * tb) * TT], in_=vtmp[vb][:], func=AF.Identity,
                        scale=sd[vb][:, 0:1], bias=sd[vb][:, 1:2]),
                        reads=[("vtmp", vb, 0), ("vtmp", vb, 1), ("sd0", vb), ("sd1", vb)],
                        writes=[("act", 24 + 2 * tb), ("act", 25 + 2 * tb)])
                release(qv0)
                release(qv1)

                for g in range(2):
                    s = acquire(q)
                    for mm in range(4):
                        m = g * 4 + mm
                        b = getbank()
                        mm_group(b, lambda kc, s=s, mm=mm: ring[s][:, (mm * 16 + kc) * 128:(mm * 16 + kc + 1) * 128],
                                 lambda kc: hs(kc), NCH,
                                 reads=[("h", c) for c in range(NCH)] + [("slot", s)])
                        S.op("act", lambda e, b=b, m=m: e.activation(out=acs(16 + m), in_=bank(b), func=AF.Gelu_apprx_tanh),
                             reads=[("ps", b)], writes=[("act", 16 + m)])
                    release(q)
                    q += 1

                stA = rms_begin()
                for hd in range(8):
                    b = getbank()
                    lh = l * 8 + hd

                    def sp_fn(e, b=b, hd=hd, lh=lh):
                        ins = None
                        for tb in range(4):
                            ins = e.matmul(bank(b)[:, tb * 128:(tb + 1) * 128],
                                           lhsT=act[:, (24 + 2 * tb) * TT + hd * 128:(24 + 2 * tb) * TT + (hd + 1) * 128],
                                           rhs=wsT[:, lh * 128:(lh + 1) * 128], start=True, stop=True)
                        return ins
                    S.op("pe", sp_fn, reads=[("act", 24 + 2 * tb + hd // 4) for tb in range(4)] + ["wsT"],
                         writes=[("ps", b)])
                    rms_flush(stA)
                    k = nxt("tmp", NTMP)
                    S.op("dve", lambda e, b=b, k=k, hd=hd, lh=lh, cb=cb: e.scalar_tensor_tensor(
                        out=tmp[k][:, 0:TT].rearrange("p (a b) -> p a b", a=4),
                        in0=bank(b).rearrange("p (a b) -> p a b", a=4),
                        scalar=col(cb + C_LNG + hd),
                        in1=E[:, lh * 128:(lh + 1) * 128].unsqueeze(1).broadcast_to([128, 4, 128]),
                        op0=ALU.mult, op1=ALU.add),
                        reads=[("ps", b), "cols", "E"], writes=[("tmp", k)])
                    k2 = nxt("tmp", NTMP)
                    S.op("dve", lambda e, k=k, k2=k2, hd=hd: e.tensor_tensor(
                        out=tmp[k2][:, 0:TT], in0=tmp[k][:, 0:TT], in1=acs(16 + hd), op=ALU.mult),
                        reads=[("tmp", k), ("act", 16 + hd)], writes=[("tmp", k2)])
                    rms_accum(stA, tmp[k2][:, 0:TT], ("tmp", k2), 8, defer=True)
                    S.op("act", lambda e, k2=k2, hd=hd, cb=cb: e.activation(
                        out=acs(hd), in_=tmp[k2][:, 0:TT], func=AF.Identity, scale=col(cb + C_GNG + hd)),
                        reads=[("tmp", k2), "cols"], writes=[("act", hd)])
                rms_finish(stA, rsA, "rsA", AW)

                stB = rms_begin()
                for j in range(8):
                    s = acquire(q)
                    bB, bC, bH = getbank(), getbank(), getbank()
                    for w, b in ((0, bB), (1, bC), (2, bH)):
                        mm_group(b, lambda kc, s=s, w=w: ring[s][:, (w * 16 + kc) * 128:(w * 16 + kc + 1) * 128],
                                 lambda kc: hs(kc), NCH,
                                 reads=[("h", c) for c in range(NCH)] + [("slot", s)])
                    release(q)
                    q += 1
                    rms_flush(stB)
                    kC, kS, kT, kY = (nxt("tmp", NTMP) for _ in range(4))
                    hj = (l * 8 + j) * 2
                    S.op("act", lambda e, kC=kC, bC=bC: e.activation(out=tmp[kC][:, 0:TT], in_=bank(bC), func=AF.Copy),
                         reads=[("ps", bC)], writes=[("tmp", kC)])
                    S.op("dve", lambda e, kS=kS, hj=hj: e.tensor_copy(out=tmp[kS][:, 0:2], in_=halo[:, hj:hj + 2]),
                         reads=[("halo", hj)], writes=[("tmp", kS)])
                    S.op("dve", lambda e, kS=kS, kC=kC, bH=bH: e.tensor_tensor(
                        out=tmp[kS][:, 2:TT + 2], in0=tmp[kC][:, 0:TT], in1=bank(bH), op=ALU.mult),
                        reads=[("tmp", kC), ("ps", bH)], writes=[("tmp", kS)])
                    S.op("dve", lambda e, kS=kS, hj=hj: e.tensor_copy(out=halo[:, hj:hj + 2], in_=tmp[kS][:, TT:TT + 2]),
                         reads=[("tmp", kS)], writes=[("halo", hj)])
                    S.op("dve", lambda e, kS=kS, kT=kT, j=j, cb=cb: e.tensor_scalar(
                        out=tmp[kT][:, 0:TT], in0=tmp[kS][:, 2:TT + 2], scalar1=col(cb + C_CW + 16 + j), scalar2=None,
                        op0=ALU.mult),
                        reads=[("tmp", kS), "cols"], writes=[("tmp", kT)])
                    S.op("dve", lambda e, kS=kS, kT=kT, j=j, cb=cb: e.scalar_tensor_tensor(
                        out=tmp[kT][:, 0:TT], in0=tmp[kS][:, 1:TT + 1], scalar=col(cb + C_CW + 8 + j), in1=tmp[kT][:, 0:TT],
                        op0=ALU.mult, op1=ALU.add),
                        reads=[("tmp", kS), "cols", ("tmp", kT)], writes=[("tmp", kT)])
                    S.op("dve", lambda e, kS=kS, kT=kT, j=j, cb=cb: e.scalar_tensor_tensor(
                        out=tmp[kT][:, 0:TT], in0=tmp[kS][:, 0:TT], scalar=col(cb + C_CW + j), in1=tmp[kT][:, 0:TT],
                        op0=ALU.mult, op1=ALU.add),
                        reads=[("tmp", kS), "cols", ("tmp", kT)], writes=[("tmp", kT)])
                    S.op("dve", lambda e, kT=kT, kY=kY, bB=bB: e.tensor_tensor(
                        out=tmp[kY][:, 0:TT], in0=tmp[kT][:, 0:TT], in1=bank(bB), op=ALU.mult),
                        reads=[("tmp", kT), ("ps", bB)], writes=[("tmp", kY)])
                    rms_accum(stB, tmp[kY][:, 0:TT], ("tmp", kY), 8, defer=True)
                    S.op("act", lambda e, kY=kY, j=j, cb=cb: e.activation(
                        out=acs(8 + j), in_=tmp[kY][:, 0:TT], func=AF.Identity, scale=col(cb + C_GNG + 8 + j)),
                        reads=[("tmp", kY), "cols"], writes=[("act", 8 + j)])
                rms_finish(stB, rsB, "rsB", AW)

                for g in range(4):
                    s = acquire(q)
                    for mm in range(4):
                        m = g * 4 + mm
                        bA, bBk = getbank(), getbank()
                        mm_group(bA, lambda kc, s=s, mm=mm: ring[s][:, (mm * 16 + kc) * 128:(mm * 16 + kc + 1) * 128],
                                 lambda kc: acs(kc), 8,
                                 reads=[("act", c) for c in range(8)] + [("slot", s)])
                        mm_group(bBk, lambda kc, s=s, mm=mm: ring[s][:, (mm * 16 + 8 + kc) * 128:(mm * 16 + 8 + kc + 1) * 128],
                                 lambda kc: acs(8 + kc), 8,
                                 reads=[("act", c) for c in range(8, 16)] + [("slot", s)])
                        k, k2 = nxt("tmp", NTMP), nxt("tmp", NTMP)
                        S.op("dve", lambda e, k=k, bA=bA: e.tensor_tensor(out=tmp[k][:, 0:TT], in0=bank(bA), in1=rsA[:], op=ALU.mult),
                             reads=[("ps", bA), "rsA"], writes=[("tmp", k)])
                        S.op("dve", lambda e, k2=k2, bBk=bBk: e.tensor_tensor(out=tmp[k2][:, 0:TT], in0=bank(bBk), in1=rsB[:], op=ALU.mult),
                             reads=[("ps", bBk), "rsB"], writes=[("tmp", k2)])
                        S.op("dve", lambda e, k=k, k2=k2: e.tensor_tensor(out=tmp[k][:, 0:TT], in0=tmp[k][:, 0:TT], in1=tmp[k2][:, 0:TT], op=ALU.add),
                             reads=[("tmp", k), ("tmp", k2)], writes=[("tmp", k)])
                        S.op("dve", lambda e, k=k, m=m: e.tensor_tensor(out=xs(m), in0=xs(m), in1=tmp[k][:, 0:TT], op=ALU.add),
                             reads=[("tmp", k), ("x", m)], writes=[("x", m)])
                    release(q)
                    q += 1

                norm_to_h(cb + C_N2G)

                for g in range(22):
                    s = acquire(q)
                    for ff in range(2):
                        f = g * 2 + ff
                        bG, bU = getbank(), getbank()
                        for w, b in ((0, bG), (1, bU)):
                            mm_group(b, lambda kc, s=s, ff=ff, w=w: ring[s][:, ((ff * 2 + w) * 16 + kc) * 128:((ff * 2 + w) * 16 + kc + 1) * 128],
                                     lambda kc: hs(kc), NCH,
                                     reads=[("h", c) for c in range(NCH)] + [("slot", s)])
                        k = nxt("tmp", NTMP)
                        S.op("act", lambda e, k=k, bG=bG: e.activation(out=tmp[k][:, 0:TT], in_=bank(bG), func=AF.Silu),
                             reads=[("ps", bG)], writes=[("tmp", k)])
                        S.op("dve", lambda e, k=k, bU=bU, f=f: e.tensor_tensor(out=acs(f), in0=tmp[k][:, 0:TT], in1=bank(bU), op=ALU.mult),
                             reads=[("tmp", k), ("ps", bU)], writes=[("act", f)])
                    release(q)
                    q += 1

                for m in range(NCH):
                    s = acquire(q)
                    b = getbank()
                    mm_group(b, lambda kc, s=s: ring[s][:, kc * 128:(kc + 1) * 128],
                             lambda kc: acs(kc), NF,
                             reads=[("act", c) for c in range(NF)] + [("slot", s)])
                    release(q)
                    q += 1
                    S.op("dve", lambda e, b=b, m=m: e.tensor_tensor(out=xs(m), in0=xs(m), in1=bank(b), op=ALU.add),
                         reads=[("ps", b), ("x", m)], writes=[("x", m)])

            if final_norm:
                stt = rms_begin()
                for c in range(NCH):
                    rms_accum(stt, xs(c), ("x", c), NCH)
                rms_finish(stt, rs1, "rs1", D)
                for c in range(NCH):
                    S.op("dve", lambda e, c=c: e.scalar_tensor_tensor(
                        out=xs(c), in0=xs(c), scalar=col(C_FNG + c), in1=rs1[:], op0=ALU.mult, op1=ALU.mult),
                        reads=[("x", c), "rs1", "cols"], writes=[("x", c)])
            S.op("sp", lambda e, ti=ti: e.dma_start(out=out[ti], in_=x[:]),
                 reads=[("x", c) for c in range(NCH)], dma="out")

        S.analyze()
        n_out = S.counts[("dma", "out")]

        @block.tensor
        def _(e):
            S.emit_engine("pe", e, sems)

        @block.scalar
        def _(e):
            S.emit_engine("act", e, sems)

        @block.vector
        def _(e):
            S.emit_engine("dve", e, sems)

        @block.gpsimd
        def _(e):
            S.emit_engine("pool", e, sems)

        @block.sync
        def _(e):
            S.emit_engine("sp", e, sems)
            e.wait_ge(sems[("dma", "out")], n_out)
    return nc


def prep_weights(w_in, w_out, w_gate, w_up, w_down):
    parts = []
    for l in range(L):
        Wk = np.asarray(w_in[l]).reshape(16, 128, 5120)
        wv = Wk[:, :, 1024:2048].reshape(16, 128, 2, 512).transpose(1, 2, 0, 3).reshape(128, -1)
        wu = Wk[:, :, 0:1024].reshape(16, 128, 2, 4, 128).transpose(1, 2, 3, 0, 4).reshape(128, -1)
        bch = Wk[:, :, 2048:5120].reshape(16, 128, 3, 8, 128).transpose(1, 3, 2, 0, 4).reshape(128, -1)
        wo = np.asarray(w_out[l]).reshape(16, 128, 4, 4, 128).transpose(1, 2, 3, 0, 4).reshape(128, -1)
        G = np.asarray(w_gate[l]).reshape(16, 128, 22, 2, 128)
        U = np.asarray(w_up[l]).reshape(16, 128, 22, 2, 128)
        ff = np.stack([G, U], axis=0).transpose(2, 3, 4, 0, 1, 5).reshape(128, -1)
        wd = np.asarray(w_down[l]).reshape(44, 128, 16, 128).transpose(1, 2, 0, 3).reshape(128, -1)
        parts += [wv, wu, bch, wo, ff, wd]
    ws = [np.ascontiguousarray(np.concatenate(parts[6 * l:6 * l + 6], axis=1), dtype=np.float32) for l in range(L)]
    assert all(w.shape == (128, LAYER_ELEMS) for w in ws)
    return ws


def prep_small(norm1_g, gmlp_ln_g, gmlp_ln_b, w_spatial, b_spatial, conv_w, group_norm_g, norm2_g, final_norm_g):
    sm = np.zeros((128, SM_TOT), np.float32)

    def colmaj(v):
        return np.asarray(v).reshape(-1, 128).T

    for l in range(L):
        cb = l * C_PER_L
        sm[:, cb + C_N1G:cb + C_N1G + 16] = colmaj(norm1_g[l])
        sm[:, cb + C_LNG:cb + C_LNG + 8] = colmaj(gmlp_ln_g[l])
        sm[:, cb + C_LNB:cb + C_LNB + 8] = colmaj(gmlp_ln_b[l])
        for k in range(3):
            sm[:, cb + C_CW + 8 * k:cb + C_CW + 8 * k + 8] = colmaj(conv_w[l, k])
        sm[:, cb + C_GNG:cb + C_GNG + 16] = colmaj(group_norm_g[l])
        sm[:, cb + C_N2G:cb + C_N2G + 16] = colmaj(norm2_g[l])
    sm[:, C_FNG:C_FNG + 16] = colmaj(final_norm_g)
    sm[:, SM_BS:SM_WS] = np.broadcast_to(np.asarray(b_spatial).reshape(1, -1), (128, L * 8 * 128))
    sm[:, SM_WS:SM_TOT] = np.asarray(w_spatial).transpose(3, 0, 1, 2).reshape(128, -1)
    return sm


def prep_x(xb, n_tiles):
    return np.ascontiguousarray(
        np.asarray(xb).reshape(n_tiles, TT, NCH, 128).transpose(0, 3, 2, 1).reshape(n_tiles, 128, NCH * TT))


def unprep_out(o, n_tiles):
    return o.reshape(n_tiles, 128, NCH, TT).transpose(0, 3, 2, 1).reshape(n_tiles * TT, D)


_NC_CACHE = {}


def kernel(x, norm1_g, w_in, gmlp_ln_g, gmlp_ln_b, w_spatial, b_spatial, conv_w, group_norm_g,
           w_out, norm2_g, w_gate, w_up, w_down, final_norm_g):
    x = np.asarray(x)
    B = x.shape[0]
    n_tiles = x.shape[1] // TT
    wsrc = prep_weights(w_in, w_out, w_gate, w_up, w_down)
    small = prep_small(norm1_g, gmlp_ln_g, gmlp_ln_b, w_spatial, b_spatial, conv_w, group_norm_g,
                       norm2_g, final_norm_g)
    key = (n_tiles,)
    if key not in _NC_CACHE:
        _NC_CACHE[key] = build(n_tiles)
    nc = _NC_CACHE[key]
    in_maps = [{"xt": prep_x(x[b], n_tiles), "wsrc0": wsrc[0], "wsrc1": wsrc[1], "small": small} for b in range(B)]
    res = run_bass_kernel_spmd(nc, in_maps, core_ids=list(range(B)))
    outs = [unprep_out(np.asarray(r["out"]), n_tiles) for r in res.results]
    return np.stack(outs, axis=0).astype(np.float32)
```
